# Optimizing a Trainium2 kernel written in Bass

```python
import jax, jax.numpy as jnp
from jax import lax
import numpy as np

D_MODEL = 1024
BATCH = 8
SEQ = 4096
DEPTH = 4

GRID_W = 64
PLE_DIM = 256
EPS = 1e-6
GLA_HEADS = 4
GLA_DK = 128
GLA_DV = 256
GLA_KEY = GLA_HEADS * GLA_DK
GLA_VAL = GLA_HEADS * GLA_DV
GLA_RANK = 16
GLA_TAU = 16.0
GLA_CHUNK = 64
ATTN_HEADS = 8
ATTN_KV_HEADS = 2
HEAD_DIM = 128
ATTN_Q = ATTN_HEADS * HEAD_DIM
ATTN_KV = ATTN_KV_HEADS * HEAD_DIM
Q_BLOCK = 128
ROPE_THETA = 10000.0
ROPE_AXIS_DIM = HEAD_DIM // 2
FFN_HIDDEN = -(-8 * D_MODEL // (3 * 256)) * 256
IN_SPLITS = (GLA_KEY, GLA_KEY, GLA_VAL, GLA_VAL, GLA_RANK, GLA_RANK,
             ATTN_Q, ATTN_KV, ATTN_KV, D_MODEL, D_MODEL)
IN_WIDTH = sum(IN_SPLITS)

kernel_name = "hybrid_gla_axialrope_gqa_encoder"


def rms_norm(x, g):
    xf = x.astype(jnp.float32)
    y = xf * lax.rsqrt(jnp.mean(xf * xf, axis=-1, keepdims=True) + EPS)
    return (y * g.astype(jnp.float32)).astype(x.dtype)


def axial_rope_tables(seq_len):
    rows = seq_len // GRID_W
    row = jnp.repeat(jnp.arange(rows, dtype=jnp.float32), GRID_W)
    col = jnp.tile(jnp.arange(GRID_W, dtype=jnp.float32), rows)
    inv = ROPE_THETA ** (-jnp.arange(0, ROPE_AXIS_DIM, 2, dtype=jnp.float32) / ROPE_AXIS_DIM)
    ang = jnp.stack([row[:, None] * inv, col[:, None] * inv], axis=1)
    return jnp.cos(ang), jnp.sin(ang)


def apply_axial_rope(x, cos, sin):
    b, s, h, d = x.shape
    xr = x.astype(jnp.float32).reshape(b, s, h, 2, 2, d // 4)
    c = cos[None, :, None]
    sn = sin[None, :, None]
    x1, x2 = xr[..., 0, :], xr[..., 1, :]
    out = jnp.stack([x1 * c - x2 * sn, x2 * c + x1 * sn], axis=-2)
    return out.reshape(b, s, h, d).astype(x.dtype)


def gla_chunked(q, k, v, log_a):
    b, s, h, dk = q.shape
    dv = v.shape[-1]
    n = s // GLA_CHUNK
    c = GLA_CHUNK

    def chunks(t):
        return t.reshape(b, n, c, h, t.shape[-1]).transpose(1, 0, 3, 2, 4)

    q, k, v, la = chunks(q), chunks(k), chunks(v), chunks(log_a)
    cum = jnp.cumsum(la, axis=3)
    last = cum[:, :, :, -1:, :]
    q_dec = q * jnp.exp(cum)
    k_intra = k * jnp.exp(-cum)
    k_state = k * jnp.exp(last - cum)
    mask = jnp.tril(jnp.ones((c, c), dtype=bool))
    scores = jnp.where(mask, jnp.einsum('nbhid,nbhjd->nbhij', q_dec, k_intra), 0.0)
    o_intra = jnp.einsum('nbhij,nbhjv->nbhiv', scores, v)

    def step(state, xs):
        qd, ks, vc, lst = xs
        o = jnp.einsum('bhid,bhdv->bhiv', qd, state)
        state = state * jnp.exp(lst[:, :, 0, :])[..., None] + jnp.einsum('bhjd,bhjv->bhdv', ks, vc)
        return state, o

    s0 = jnp.zeros((b, h, dk, dv), jnp.float32)
    _, o_inter = lax.scan(step, s0, (q_dec, k_state, v, last))
    o = o_intra + o_inter
    return o.transpose(1, 0, 3, 2, 4).reshape(b, s, h, dv)


def gqa_attention(q, k, v):
    b, s, hq, hd = q.shape
    hkv = k.shape[2]
    g = hq // hkv
    nb = s // Q_BLOCK
    qb = q.reshape(b, nb, Q_BLOCK, hkv, g, hd).transpose(1, 0, 2, 3, 4, 5)
    scale = hd ** -0.5

    def block(qblk):
        sc = jnp.einsum('bqkgd,bskd->bkgqs', qblk, k).astype(jnp.float32) * scale
        pr = jax.nn.softmax(sc, axis=-1).astype(v.dtype)
        return jnp.einsum('bkgqs,bskd->bqkgd', pr, v)

    out = lax.map(block, qb)
    return out.transpose(1, 0, 2, 3, 4, 5).reshape(b, s, hq * hd)


def hybrid_layer(h, p_i, cos, sin, g_mix_pre, w_in, w_alpha_up, b_alpha, g_gla_out,
                 g_q_norm, g_k_norm, w_o_gla, w_o_attn, w_out, g_mix_post,
                 g_ffn_pre, w_ffn_in, w_ffn_out, g_ffn_post, w_ple_proj, w_ple_gate, g_ple_post):
    b, s, _ = h.shape
    f32 = jnp.float32
    u = rms_norm(h, g_mix_pre)
    offsets = np.cumsum(IN_SPLITS)[:-1].tolist()
    (gq, gk, gv, gg, ra_f, ra_b, aq, ak, av, gate_a, gate_b) = jnp.split(u @ w_in, offsets, axis=-1)

    qh = gq.astype(f32).reshape(b, s, GLA_HEADS, GLA_DK) * (GLA_DK ** -0.5)
    kh = gk.astype(f32).reshape(b, s, GLA_HEADS, GLA_DK)
    vh = gv.astype(f32).reshape(b, s, GLA_HEADS, GLA_DV)
    la_f = (jax.nn.log_sigmoid(ra_f.astype(f32) @ w_alpha_up[0].astype(f32) + b_alpha[0].astype(f32))
            / GLA_TAU).reshape(b, s, GLA_HEADS, GLA_DK)
    la_b = (jax.nn.log_sigmoid(ra_b.astype(f32) @ w_alpha_up[1].astype(f32) + b_alpha[1].astype(f32))
            / GLA_TAU).reshape(b, s, GLA_HEADS, GLA_DK)
    o_f = gla_chunked(qh, kh, vh, la_f)
    o_b = gla_chunked(qh[:, ::-1], kh[:, ::-1], vh[:, ::-1], la_b[:, ::-1])[:, ::-1]
    o_gla = rms_norm(o_f + o_b, g_gla_out).reshape(b, s, GLA_VAL).astype(h.dtype)
    branch_a = (o_gla * jax.nn.silu(gg)) @ w_o_gla

    q = apply_axial_rope(rms_norm(aq.reshape(b, s, ATTN_HEADS, HEAD_DIM), g_q_norm), cos, sin)
    k = apply_axial_rope(rms_norm(ak.reshape(b, s, ATTN_KV_HEADS, HEAD_DIM), g_k_norm), cos, sin)
    v = av.reshape(b, s, ATTN_KV_HEADS, HEAD_DIM)
    branch_b = gqa_attention(q, k, v) @ w_o_attn

    mixed = jax.nn.sigmoid(gate_a) * branch_a + jax.nn.sigmoid(gate_b) * branch_b
    h = h + rms_norm(mixed @ w_out, g_mix_post)

    gate, up = jnp.split(rms_norm(h, g_ffn_pre) @ w_ffn_in, 2, axis=-1)
    h = h + rms_norm((jax.nn.silu(gate) * up) @ w_ffn_out, g_ffn_post)

    e = p_i @ w_ple_proj
    h = h + rms_norm(jax.nn.sigmoid(h @ w_ple_gate) * e, g_ple_post)
    return h


def setup_inputs(seed: int = 0) -> dict:
    key = jax.random.key(seed)
    ks = jax.random.split(key, 24)
    f32 = jnp.float32

    def w(k, shape, fan_in):
        return jax.random.normal(k, shape, f32) * (fan_in ** -0.5)

    def gain(k, shape):
        return 1.0 + 0.05 * jax.random.normal(k, shape, f32)

    return {
        "x": jax.random.normal(ks[0], (BATCH, SEQ, D_MODEL), f32),
        "p": jax.random.normal(ks[1], (DEPTH, BATCH, SEQ, PLE_DIM), f32),
        "g_mix_pre": gain(ks[2], (DEPTH, D_MODEL)),
        "w_in": w(ks[3], (DEPTH, D_MODEL, IN_WIDTH), D_MODEL),
        "w_alpha_up": w(ks[4], (DEPTH, 2, GLA_RANK, GLA_KEY), GLA_RANK),
        "b_alpha": 1.0 + 0.1 * jax.random.normal(ks[5], (DEPTH, 2, GLA_KEY), f32),
        "g_gla_out": gain(ks[6], (DEPTH, GLA_DV)),
        "g_q_norm": gain(ks[7], (DEPTH, HEAD_DIM)),
        "g_k_norm": gain(ks[8], (DEPTH, HEAD_DIM)),
        "w_o_gla": w(ks[9], (DEPTH, GLA_VAL, D_MODEL), GLA_VAL),
        "w_o_attn": w(ks[10], (DEPTH, ATTN_Q, D_MODEL), ATTN_Q),
        "w_out": w(ks[11], (DEPTH, D_MODEL, D_MODEL), D_MODEL),
        "g_mix_post": gain(ks[12], (DEPTH, D_MODEL)),
        "g_ffn_pre": gain(ks[13], (DEPTH, D_MODEL)),
        "w_ffn_in": w(ks[14], (DEPTH, D_MODEL, 2 * FFN_HIDDEN), D_MODEL),
        "w_ffn_out": w(ks[15], (DEPTH, FFN_HIDDEN, D_MODEL), FFN_HIDDEN),
        "g_ffn_post": gain(ks[16], (DEPTH, D_MODEL)),
        "w_ple_proj": w(ks[17], (DEPTH, PLE_DIM, D_MODEL), PLE_DIM),
        "w_ple_gate": w(ks[18], (DEPTH, D_MODEL, D_MODEL), D_MODEL),
        "g_ple_post": gain(ks[19], (DEPTH, D_MODEL)),
    }


def reference(x, p, g_mix_pre, w_in, w_alpha_up, b_alpha, g_gla_out, g_q_norm, g_k_norm,
              w_o_gla, w_o_attn, w_out, g_mix_post, g_ffn_pre, w_ffn_in, w_ffn_out,
              g_ffn_post, w_ple_proj, w_ple_gate, g_ple_post):
    cos, sin = axial_rope_tables(x.shape[1])
    h = x
    for i in range(DEPTH):
        h = hybrid_layer(h, p[i], cos, sin, g_mix_pre[i], w_in[i], w_alpha_up[i], b_alpha[i],
                         g_gla_out[i], g_q_norm[i], g_k_norm[i], w_o_gla[i], w_o_attn[i],
                         w_out[i], g_mix_post[i], g_ffn_pre[i], w_ffn_in[i], w_ffn_out[i],
                         g_ffn_post[i], w_ple_proj[i], w_ple_gate[i], g_ple_post[i])
    return h
```

```python
import numpy as np
import concourse.bass as bass
import concourse.mybir as mybir
from concourse.bass_utils import run_bass_kernel_spmd

F32 = mybir.dt.float32
BF16 = mybir.dt.bfloat16
AF = mybir.ActivationFunctionType
ALU = mybir.AluOpType

T = 4096
D = 1024
NL = 4
TB = 512
NTB = T // TB
PLE = 256
FH = 2816
INW = 6688
EPS = 1e-6
C_GQ, C_GK, C_GV, C_GG, C_RA, C_AQ, C_AK, C_AV, C_GA, C_GB = (
    0, 512, 1024, 2048, 3072, 3104, 4128, 4384, 4640, 5664)

ENGS = ("pe", "act", "dve", "pool", "sp")
DMA_SLOTS = {"sp": 16, "act": 4, "pool": 8}
SAME_ENG_SYNC = True


class Prog:
    def __init__(self, nc):
        self.nc = nc
        self.ops = []
        self.last_w = {}
        self.readers = {}
        self.last_eng = {}
        self.dma_last = {}
        self.dcnt = {q: 0 for q in DMA_SLOTS}
        self.bar = None
        self.bar_pending = set()

    def barrier(self):
        s = set(self.last_eng.values())
        s.update(self.dma_last.values())
        self.bar = s
        self.bar_pending = set(ENGS)
        self.last_w = {}
        self.readers = {}

    def add(self, eng, fn, r=(), w=(), dma=False):
        idx = len(self.ops)
        deps = set()
        last_w = self.last_w
        readers = self.readers
        for b in r:
            lw = last_w.get(b)
            if lw is not None:
                deps.add(lw)
        for b in w:
            lw = last_w.get(b)
            if lw is not None:
                deps.add(lw)
            rd = readers.get(b)
            if rd:
                deps.update(rd.values())
        if eng in self.bar_pending:
            deps.update(self.bar)
            self.bar_pending.discard(eng)
        for b in w:
            last_w[b] = idx
            readers[b] = {}
        key = ("d", idx) if dma else eng
        for b in r:
            rd = readers.get(b)
            if rd is None:
                rd = readers[b] = {}
            rd[key] = idx
        if dma:
            j = self.dcnt[eng]
            self.dcnt[eng] = j + 1
            self.dma_last[(eng, j % DMA_SLOTS[eng])] = idx
        self.last_eng[eng] = idx
        self.ops.append([eng, fn, dma, deps])
        return idx

    def pe(self, r, w, meth, *a, **k):
        return self.add("pe", (meth, a, k), r, w)

    def act(self, r, w, meth, *a, **k):
        return self.add("act", (meth, a, k), r, w)

    def dve(self, r, w, meth, *a, **k):
        return self.add("dve", (meth, a, k), r, w)

    def pool(self, r, w, meth, *a, **k):
        return self.add("pool", (meth, a, k), r, w)

    def dma(self, out, in_, r=(), w=(), q="sp"):
        return self.add(q, ("dma_start", (), dict(out=out, in_=in_)), r, w, dma=True)

    def emit(self):
        nc = self.nc
        ops = self.ops
        n = len(ops)
        need = [False] * n
        for i, (eng, fn, dma, deps) in enumerate(ops):
            for d in deps:
                de, _, ddma, _ = ops[d]
                if ddma:
                    need[d] = True
                elif de == eng and not dma:
                    if eng != "pe" and SAME_ENG_SYNC:
                        need[d] = True
                else:
                    need[d] = True
        esem = {e: nc.alloc_semaphore(name=f"sem_{e}") for e in ENGS}
        dsem = {q: [nc.alloc_semaphore(name=f"dsem_{q}_{k}") for k in range(K)]
                for q, K in DMA_SLOTS.items()}
        ecount = {e: 0 for e in ENGS}
        dcount = {q: 0 for q in DMA_SLOTS}
        sig = [None] * n
        prevslot = [None] * n
        for i, (eng, fn, dma, deps) in enumerate(ops):
            if dma:
                j = dcount[eng]
                dcount[eng] += 1
                K = DMA_SLOTS[eng]
                slot = j % K
                rnd = j // K
                sig[i] = (dsem[eng][slot], 16 * (rnd + 1), 16)
                if rnd > 0:
                    prevslot[i] = (dsem[eng][slot], 16 * rnd)
            elif need[i]:
                ecount[eng] += 1
                sig[i] = (esem[eng], ecount[eng], 1)
        waited = {e: {} for e in ENGS}
        per_eng = {e: [] for e in ENGS}
        nwaits = 0
        for i, (eng, fn, dma, deps) in enumerate(ops):
            wl = {}
            if prevslot[i] is not None:
                s, v = prevslot[i]
                wl[s] = v
            for d in deps:
                de, _, ddma, _ = ops[d]
                if not ddma and de == eng and not dma and (eng == "pe" or not SAME_ENG_SYNC):
                    continue
                s, v, _ = sig[d]
                if v > wl.get(s, 0):
                    wl[s] = v
            wt = waited[eng]
            final = []
            for s, v in wl.items():
                if wt.get(s, 0) >= v:
                    continue
                wt[s] = v
                final.append((s, v))
            nwaits += len(final)
            per_eng[eng].append((final, fn, sig[i]))
        self.stats = dict(n_ops=n, n_waits=nwaits, per_eng={e: len(v) for e, v in per_eng.items()},
                          ecount=dict(ecount), dcount=dict(dcount))

        def run(e, lst):
            for waits, fn, sg in lst:
                for s, v in waits:
                    e.wait_ge(s, v)
                meth, a, k = fn
                ins = getattr(e, meth)(*a, **k)
                if sg is not None:
                    ins.then_inc(sg[0], sg[2])

        tail = []
        for q in DMA_SLOTS:
            j = dcount[q]
            K = DMA_SLOTS[q]
            for slot in range(min(j, K)):
                cnt = (j - 1 - slot) // K + 1
                tail.append((dsem[q][slot], 16 * cnt))

        with nc.Block() as block:
            @block.tensor
            def _(e):
                run(e, per_eng["pe"])

            @block.scalar
            def _(e):
                run(e, per_eng["act"])

            @block.vector
            def _(e):
                run(e, per_eng["dve"])

            @block.gpsimd
            def _(e):
                run(e, per_eng["pool"])

            @block.sync
            def _(e):
                run(e, per_eng["sp"])
                for s, v in tail:
                    e.wait_ge(s, v)


class Ring:
    def __init__(self, tiles, name):
        self.tiles = tiles
        self.name = name
        self.i = 0

    def next(self):
        k = self.i % len(self.tiles)
        self.i += 1
        return self.tiles[k], (self.name, k)


def make_consts():
    j = np.arange(128)[:, None]
    i = np.arange(128)[None, :]
    c = {}
    s = -1.0 / 16.0
    tri = np.zeros((128, 8, 128), np.float32)
    tri[:, 0] = (j <= i) * s
    tri[:, 1] = (j > i) * s
    tri[:, 2] = (j <= i) * 1.0
    tri[:, 3] = (j >= i) * s
    tri[:, 4] = (j < i) * s
    tri[:, 5] = (j >= i) * 1.0
    tri[:, 6] = np.eye(128)
    rot = np.zeros((128, 128), np.float32)
    for d in range(128):
        if d % 64 < 32:
            rot[d + 32, d] = -1.0
        else:
            rot[d - 32, d] = 1.0
    tri[:, 7] = rot
    c["ctri"] = tri
    t = np.arange(T)
    row = (t // 64).astype(np.float32)
    col = (t % 64).astype(np.float32)
    inv = (10000.0 ** (-np.arange(0, 64, 2, dtype=np.float32) / 64.0)).astype(np.float32)
    ang = np.zeros((128, T), np.float32)
    for d in range(128):
        pos = row if d < 64 else col
        ang[d] = pos * inv[d % 32]
    c["crope"] = np.stack([np.cos(ang), np.sin(ang)], axis=1).astype(np.float32)
    return c


def build(n_layers=NL, debug=()):
    nc = bass.Bass("TRN2", target_bir_lowering=False)
    P = Prog(nc)

    def din(name, shape, dt=F32):
        return nc.dram_tensor(name, list(shape), dt, kind="ExternalInput").ap()

    def dscr(name, shape, dt):
        kind = "ExternalOutput" if name in debug else "Internal"
        return nc.dram_tensor(name, list(shape), dt, kind=kind).ap()

    x_in = din("x", [T, D])
    p_in = din("p", [NL, T, PLE])
    g_mix_pre = din("g_mix_pre", [NL, D])
    w_in = din("w_in", [NL, D, INW])
    w_alpha_up = din("w_alpha_up", [NL, 2, 16, 512])
    b_alpha = din("b_alpha", [NL, 2, 512])
    g_gla_out = din("g_gla_out", [NL, 256])
    g_q_norm = din("g_q_norm", [NL, 128])
    g_k_norm = din("g_k_norm", [NL, 128])
    w_o_gla = din("w_o_gla", [NL, D, D])
    w_o_attn = din("w_o_attn", [NL, D, D])
    w_out = din("w_out", [NL, D, D])
    g_mix_post = din("g_mix_post", [NL, D])
    g_ffn_pre = din("g_ffn_pre", [NL, D])
    w_ffn_in = din("w_ffn_in", [NL, D, 2 * FH])
    w_ffn_out = din("w_ffn_out", [NL, FH, D])
    g_ffn_post = din("g_ffn_post", [NL, D])
    w_ple_proj = din("w_ple_proj", [NL, PLE, D])
    w_ple_gate = din("w_ple_gate", [NL, D, D])
    g_ple_post = din("g_ple_post", [NL, D])
    ctri = din("ctri", [128, 8, 128])
    crope = din("crope", [128, 2, T])
    out = nc.dram_tensor("out", [T, D], F32, kind="ExternalOutput").ap()

    hT = dscr("hT", [8, 128, T], F32)
    Yd = dscr("Yd", [8, 128, T], F32)
    QG = dscr("QG", [4, 128, T], F32)
    KG = dscr("KG", [4, 128, T], F32)
    KGt = dscr("KGt", [T, 512], F32)
    VGt = dscr("VGt", [T, 1024], BF16)
    GG = dscr("GG", [8, 128, T], BF16)
    RA = dscr("RA", [32, T], BF16)
    AQ = dscr("AQ", [8, 128, T], BF16)
    AK = dscr("AK", [2, 128, T], BF16)
    AVt = dscr("AVt", [T, 256], BF16)
    SGA = dscr("SGA", [8, 128, T], BF16)
    SGB = dscr("SGB", [8, 128, T], BF16)
    OB = dscr("OB", [8, 128, T], F32)
    GA = dscr("GA", [8, 128, T], BF16)
    AT = dscr("AT", [8, 128, T], BF16)
    MXT = dscr("MXT", [8, 128, T], F32)
    MX = dscr("MX", [8, 128, T], BF16)
    HID = dscr("HID", [22, 128, T], BF16)

    ps = [nc.alloc_psum_tensor(f"ps{i}", [128, 512], F32).ap() for i in range(8)]

    CT = nc.alloc_sbuf_tensor("CT", [128, 8, 128], F32).ap()
    ONESF = nc.alloc_sbuf_tensor("ONESF", [128, 128], F32).ap()
    ONESB = nc.alloc_sbuf_tensor("ONESB", [128, 128], BF16).ap()
    GV = nc.alloc_sbuf_tensor("GV", [128, 64], F32).ap()
    ARENA_KB = 198
    arena = nc.alloc_sbuf_tensor("arena", [128, ARENA_KB * 256], F32).ap()
    IDN = CT[:, 6, :]
    ROT = CT[:, 7, :]

    class Carver:
        def __init__(self, lo_kb, hi_kb):
            self.off = lo_kb * 1024
            self.hi = hi_kb * 1024

        def __call__(self, shape, dt, parts=128):
            n = 1
            for d in shape:
                n *= d
            esz = 4 if dt == F32 else 2
            nbytes = (n * esz + 63) // 64 * 64
            assert self.off + nbytes <= self.hi, ("arena overflow", self.off, nbytes, self.hi)
            v = arena[0:parts, self.off // 4:(self.off + n * esz) // 4]
            self.off += nbytes
            if dt == BF16:
                v = v.bitcast(BF16)
            if len(shape) == 1:
                return v
            names = " ".join(f"d{i}" for i in range(len(shape)))
            kw = {f"d{i}": shape[i] for i in range(len(shape) - 1)}
            return v.rearrange(f"p ({names}) -> p {names}", **kw)

    top = Carver(ARENA_KB - 112, ARENA_KB)
    XR = top([32768], BF16)
    WS = [top([4096], F32) for _ in range(2)]
    WB = [top([4096], BF16) for _ in range(2)]
    LOW_KB = ARENA_KB - 112

    P.dma(CT, ctri, w=["CT"])
    P.pool([], ["ONESF"], "memset", ONESF, 1.0)
    P.pool([], ["ONESB"], "memset", ONESB, 1.0)

    X8 = XR[:, 0:8 * T].rearrange("p (c t) -> p c t", c=8)

    state = {"bank": 0, "ws": 0, "wb": 0}

    def nbank():
        b = state["bank"] % 8
        state["bank"] += 1
        return b

    def nbank2():
        b = state["bank"] % 8
        if b % 2:
            b = (b + 1) % 8
        state["bank"] = b + 2
        return b

    def flat(ap3):
        n = len(ap3.shape)
        if n == 2:
            return ap3
        if n == 3:
            return ap3.rearrange("p a b -> p (a b)")
        return ap3.rearrange("p a b c -> p (a b c)")

    GCOL = {"mix_pre": 0, "mix_post": 8, "ffn_pre": 16, "ffn_post": 24, "ple_post": 32,
            "mix_pre_next": 40, "gla": 48, "qn": 50, "kn": 51}

    def load_gains(l):
        def ld(col, src, n):
            for c in range(n):
                P.dma(GV[:, col + c:col + c + 1], src[c * 128:(c + 1) * 128].rearrange("(p o) -> p o", o=1), w=["GV"])
        ld(0, g_mix_pre[l], 8)
        ld(8, g_mix_post[l], 8)
        ld(16, g_ffn_pre[l], 8)
        ld(24, g_ffn_post[l], 8)
        ld(32, g_ple_post[l], 8)
        if l + 1 < NL:
            ld(40, g_mix_pre[l + 1], 8)
        ld(48, g_gla_out[l], 2)
        ld(50, g_q_norm[l], 1)
        ld(51, g_k_norm[l], 1)

    def load_slab(pieces, KC, ncols, dst=None, dstbuf=None):
        s = state["ws"] % 2
        state["ws"] += 1
        ws = WS[s][:, 0:KC * ncols].rearrange("p (c n) -> p c n", c=KC)
        off = 0
        for wap in pieces:
            n_i = wap.shape[2]
            P.dma(ws[:, :, off:off + n_i], wap, w=[("WS", s)])
            off += n_i
        if dst is None:
            b = state["wb"] % 2
            state["wb"] += 1
            wb = WB[b][:, 0:KC * ncols].rearrange("p (c n) -> p c n", c=KC)
            P.pool([("WS", s)], [("WB", b)], "tensor_copy", out=wb, in_=ws)
            return wb, ("WB", b)
        P.pool([("WS", s)], [dstbuf], "tensor_copy", out=dst, in_=ws)
        return dst, dstbuf

    def run_slabs(specs):
        nxt = load_slab(*specs[0][:3])
        for i, sp in enumerate(specs):
            cur = nxt
            if i + 1 < len(specs):
                nxt = load_slab(*specs[i + 1][:3])
            sp[3](*cur)

    def wview(w2d):
        return w2d.rearrange("(c p) n -> p c n", p=128)

    def mm_group(bank, M, wb, wbuf, c0, KC, xfn, tb, ntok=TB):
        for kc in range(KC):
            xa, xb = xfn(kc, tb)
            P.pe([wbuf, xb], [("ps", bank)], "matmul", ps[bank][0:M, 0:ntok], lhsT=wb[:, kc, c0:c0 + M], rhs=xa,
                 start=(kc == 0), stop=(kc == KC - 1))

    def x8fn(kc, tb):
        return X8[:, kc, tb * TB:(tb + 1) * TB], ("X", kc, tb)

    def rstd_from(bank, rs, rsb, n):
        P.act([("ps", bank)], [rsb], "activation", out=rs, in_=ps[bank], func=AF.Sqrt, scale=1.0 / n, bias=EPS)
        P.dve([rsb], [rsb], "reciprocal", out=rs, in_=rs)

    def norm_pass(l, y_gcol, next_gcol, first=False, last=False, raw_next=False):
        cv = Carver(0, LOW_KB)
        Hs_a = cv([2, 8, TB], F32)
        Ys = cv([8, TB], F32)
        SQ = cv([8, TB], F32)
        RS_a = cv([2, TB], F32)
        XO_a = cv([2, D], F32)
        hTv = hT.rearrange("c p t -> p c t")
        Ydv = Yd.rearrange("c p t -> p c t")
        for tb in range(NTB):
            k2 = tb % 2
            Hs = Hs_a[:, k2]
            hb = ("npH", k2)
            yb = "npY"
            tsl = slice(tb * TB, (tb + 1) * TB)
            if first:
                for tt in range(4):
                    xt = XO_a[:, tt % 2]
                    xtb = ("npXO", tt % 2)
                    r0 = tb * TB + tt * 128
                    P.dma(xt, x_in[r0:r0 + 128, :], w=[xtb])
                    for kc in range(8):
                        P.pe([xtb, "CT"], [("ps", kc)], "transpose",
                             ps[kc][:, tt * 128:(tt + 1) * 128], xt[:, kc * 128:(kc + 1) * 128], IDN)
                for kc in range(8):
                    P.act([("ps", kc)], [hb], "copy", out=Hs[:, kc, :], in_=ps[kc])
            else:
                P.dma(Hs, hTv[:, :, tsl], r=["hT"], w=[hb])

            def stats(src, srcb, rsk):
                P.act([srcb], ["npSQ"], "activation", out=flat(SQ), in_=flat(src), func=AF.Square)
                bank = nbank()
                for kc in range(8):
                    P.pe(["ONESF", "npSQ"], [("ps", bank)], "matmul", ps[bank], lhsT=ONESF, rhs=SQ[:, kc, :],
                         start=(kc == 0), stop=(kc == 7))
                rs = RS_a[:, rsk]
                rstd_from(bank, rs, ("npRS", rsk), D)
                return rs, ("npRS", rsk)

            if y_gcol is not None:
                P.dma(Ys, Ydv[:, :, tsl], r=["Yd"], w=[yb])
                rs, rsb = stats(Ys, yb, 0)
                for kc in range(8):
                    P.dve([yb, rsb, "GV"], [yb], "scalar_tensor_tensor", out=Ys[:, kc, :], in0=Ys[:, kc, :],
                          scalar=GV[:, y_gcol + kc:y_gcol + kc + 1], in1=rs, op0=ALU.mult, op1=ALU.mult)
                P.pool([hb, yb], [hb], "tensor_tensor", out=flat(Hs), in0=flat(Hs), in1=flat(Ys), op=ALU.add)
            if last:
                for tt in range(4):
                    ot = XO_a[:, tt % 2]
                    otb = ("npXO", tt % 2)
                    b0 = nbank2()
                    for kc in range(8):
                        bk = b0 + kc // 4
                        P.pe([hb, "CT"], [("ps", bk)], "transpose",
                             ps[bk][:, (kc % 4) * 128:(kc % 4 + 1) * 128], Hs[:, kc, tt * 128:(tt + 1) * 128], IDN)
                    P.act([("ps", b0)], [otb], "copy", out=ot[:, 0:512], in_=ps[b0])
                    P.dve([("ps", b0 + 1)], [otb], "tensor_copy", out=ot[:, 512:1024], in_=ps[b0 + 1])
                    r0 = tb * TB + tt * 128
                    P.dma(out[r0:r0 + 128, :], ot, r=[otb], w=["out"])
                continue
            P.dma(hTv[:, :, tsl], Hs, r=[hb], w=["hT"])
            xw = [("X", kc, tb) for kc in range(8)]
            if raw_next:
                P.act([hb], xw, "copy", out=X8[:, :, tsl], in_=Hs)
            else:
                rs, rsb = stats(Hs, hb, 1)
                for kc in range(8):
                    P.dve([hb, rsb, "GV"], [("X", kc, tb)], "scalar_tensor_tensor", out=X8[:, kc, tsl], in0=Hs[:, kc, :],
                          scalar=GV[:, next_gcol + kc:next_gcol + kc + 1], in1=rs, op0=ALU.mult, op1=ALU.mult)

    def inproj(l):
        Wv = wview(w_in[l])
        cv = Carver(0, LOW_KB)
        Ef = Ring([cv([TB], F32) for _ in range(4)], "ipE")
        Eb = Ring([cv([TB], BF16) for _ in range(4)], "ipEb")
        Rr = Ring([cv([2, TB], F32) for _ in range(2)], "ipR")
        sq, y, t2, rs, t1 = (cv([TB], F32) for _ in range(5))
        WT = cv([8, 1792], BF16)
        TOa = cv([2, 512], F32)
        TOba = cv([2, 1280], BF16)

        specs = []

        def seg_simple(col0, ncols_total, dst, kind, M=128):
            done_ = 0
            ti0 = 0
            while done_ < ncols_total:
                nsl = min(512, ncols_total - done_)

                def body(wb, wbuf, nsl=nsl, ti0=ti0):
                    ti = ti0
                    for c0 in range(0, nsl, M):
                        for tb in range(NTB):
                            bank = nbank()
                            mm_group(bank, M, wb, wbuf, c0, 8, x8fn, tb)
                            tsl = slice(tb * TB, (tb + 1) * TB)
                            if kind in ("q", "k"):
                                e_, eb_ = Ef.next()
                                if kind == "q":
                                    P.act([("ps", bank)], [eb_], "mul", out=e_, in_=ps[bank], mul=128.0 ** -0.5)
                                else:
                                    P.dve([("ps", bank)], [eb_], "tensor_copy", out=e_, in_=ps[bank])
                                P.dma(dst[ti, :, tsl], e_, r=[eb_], w=[dst.name])
                            else:
                                e_, eb_ = Eb.next()
                                fn = {"silu": AF.Silu, "sig": AF.Sigmoid, "ra": AF.Copy}[kind]
                                P.act([("ps", bank)], [eb_], "activation", out=e_[0:M, :], in_=ps[bank][0:M, :], func=fn)
                                if kind == "ra":
                                    P.dma(dst[:, tsl], e_[0:M, :], r=[eb_], w=[dst.name])
                                else:
                                    P.dma(dst[ti, :, tsl], e_, r=[eb_], w=[dst.name])
                        ti += 1
                specs.append(([Wv[:, :, col0 + done_:col0 + done_ + nsl]], 8, nsl, body))
                ti0 += nsl // M
                done_ += nsl

        def seg_rope(col0, nheads, dst, gcol):
            done_ = 0
            ti0 = 0
            ncols_total = nheads * 128
            while done_ < ncols_total:
                nsl = min(512, ncols_total - done_)

                def body(wb, wbuf, nsl=nsl, ti0=ti0):
                    ti = ti0
                    for c0 in range(0, nsl, 128):
                        for tb in range(NTB):
                            tsl = slice(tb * TB, (tb + 1) * TB)
                            bank = nbank()
                            mm_group(bank, 128, wb, wbuf, c0, 8, x8fn, tb)
                            rp, rpb = Rr.next()
                            P.dma(rp, crope[:, :, tsl], w=[rpb])
                            P.act([("ps", bank)], ["ipSQ"], "activation", out=sq, in_=ps[bank], func=AF.Square)
                            b2 = nbank()
                            P.pe(["ONESF", "ipSQ"], [("ps", b2)], "matmul", ps[b2], lhsT=ONESF, rhs=sq, start=True, stop=True)
                            rstd_from(b2, rs, "ipRS", 128)
                            P.dve([("ps", bank), "ipRS", "GV"], ["ipY"], "scalar_tensor_tensor", out=y, in0=ps[bank],
                                  scalar=GV[:, gcol:gcol + 1], in1=rs, op0=ALU.mult, op1=ALU.mult)
                            b3 = nbank()
                            P.pe(["CT", "ipY"], [("ps", b3)], "matmul", ps[b3], lhsT=ROT, rhs=y, start=True, stop=True)
                            P.pool(["ipY", rpb], ["ipT2"], "tensor_tensor", out=t2, in0=y, in1=rp[:, 0, :], op=ALU.mult)
                            P.dve([("ps", b3), rpb], ["ipT1"], "tensor_tensor", out=t1, in0=ps[b3], in1=rp[:, 1, :], op=ALU.mult)
                            e_, eb_ = Eb.next()
                            P.pool(["ipT1", "ipT2"], [eb_], "tensor_tensor", out=e_, in0=t1, in1=t2, op=ALU.add)
                            P.dma(dst[ti, :, tsl], e_, r=[eb_], w=[dst.name])
                        ti += 1
                specs.append(([Wv[:, :, col0 + done_:col0 + done_ + nsl]], 8, nsl, body))
                ti0 += nsl // 128
                done_ += nsl

        seg_simple(C_GQ, 512, QG, "q")
        seg_simple(C_GK, 512, KG, "k")
        seg_simple(C_GG, 1024, GG, "silu")
        seg_simple(C_RA, 32, RA, "ra", M=32)
        seg_rope(C_AQ, 8, AQ, GCOL["qn"])
        seg_rope(C_AK, 2, AK, GCOL["kn"])
        seg_simple(C_GA, 1024, SGA, "sig")
        seg_simple(C_GB, 1024, SGB, "sig")

        off = 0
        for (c0, n) in ((C_GK, 512), (C_GV, 512), (C_GV + 512, 512), (C_AV, 256)):
            def body(wb, wbuf, off=off, n=n):
                P.act([wbuf], ["ipWT"], "copy", out=WT[:, :, off:off + n], in_=wb)
            specs.append(([Wv[:, :, c0:c0 + n]], 8, n, body))
            off += n
        run_slabs(specs)
        for tt in range(T // 128):
            k2 = tt % 2
            to = TOa[:, k2]
            tob = TOba[:, k2]
            tb, sub = tt // 4, tt % 4
            banks = []
            for (o0, n) in ((0, 512), (512, 512), (1024, 512), (1536, 256)):
                bank = nbank()
                banks.append(bank)
                for kc in range(8):
                    P.pe(["ipWT", ("X", kc, tb)], [("ps", bank)], "matmul", ps[bank][:, 0:n],
                         lhsT=X8[:, kc, tb * TB + sub * 128: tb * TB + (sub + 1) * 128],
                         rhs=WT[:, kc, o0:o0 + n], start=(kc == 0), stop=(kc == 7))
            P.dve([("ps", banks[0])], [("ipTO", k2)], "tensor_copy", out=to, in_=ps[banks[0]])
            P.act([("ps", banks[1])], [("ipTOb", k2)], "copy", out=tob[:, 0:512], in_=ps[banks[1]])
            P.dve([("ps", banks[2])], [("ipTOb", k2)], "tensor_copy", out=tob[:, 512:1024], in_=ps[banks[2]])
            P.act([("ps", banks[3])], [("ipTOb", k2)], "copy", out=tob[:, 1024:1280], in_=ps[banks[3]][:, 0:256])
            r0 = tt * 128
            P.dma(KGt[r0:r0 + 128, :], to, r=[("ipTO", k2)], w=["KGt"])
            P.dma(VGt[r0:r0 + 128, :], tob[:, 0:1024], r=[("ipTOb", k2)], w=["VGt"])
            P.dma(AVt[r0:r0 + 128, :], tob[:, 1024:1280], r=[("ipTOb", k2)], w=["AVt"])

    def gla_sweep(l, dirn):
        TI = CT[:, 3 * dirn + 0, :]
        TS = CT[:, 3 * dirn + 1, :]
        MK = CT[:, 3 * dirn + 2, :]
        cv = Carver(0, ARENA_KB)
        WU = cv([512], BF16)
        WUs = cv([512], F32)
        WUl = cv([512], F32)
        RAT = cv([2, TB], BF16)
        QT = cv([2, 4, TB], F32)
        KT = cv([2, 4, TB], F32)
        Kt = cv([2, 4, 512], F32)
        Vt = cv([2, 4, 1024], BF16)
        SPa, EDa, EPa, EMa = (cv([2, 512], F32) for _ in range(4))
        ksta, qda, kia, sca = (cv([2, 512], BF16) for _ in range(4))
        S = cv([1024], F32)
        Sba = cv([2, 1024], BF16)
        OTa = cv([2, 8, TB], F32)
        OBs = cv([8, TB], F32)
        SG = cv([8, TB], BF16)
        RSa = cv([4, TB], F32)
        GO = cv([8, TB], BF16)

        P.pool([], ["glWUs"], "memset", WUs, 0.0)
        P.dma(WUs[16 * dirn:16 * dirn + 16, :], w_alpha_up[l, dirn], r=[], w=["glWUs"])
        P.dma(WUs[32:33, :], b_alpha[l, dirn:dirn + 1, :], w=["glWUs"])
        P.dma(WUs[64:65, :], b_alpha[l, dirn:dirn + 1, :], w=["glWUs"])
        P.dve(["glWUs"], ["glWU"], "tensor_copy", out=WU[0:65, :], in_=WUs[0:65, :])
        P.dve(["glWU"], ["glWUl"], "tensor_copy", out=WUl[64:65, :], in_=WU[64:65, :])
        P.dve(["glWUs", "glWUl"], ["glWUl"], "tensor_tensor", out=WUl[64:65, :], in0=WUs[64:65, :], in1=WUl[64:65, :],
              op=ALU.subtract)
        P.dve(["glWUl"], ["glWU"], "tensor_copy", out=WU[64:65, :], in_=WUl[64:65, :])
        ratw = [("glRAT", 0), ("glRAT", 1)]
        P.pool([], ratw, "memset", flat(RAT), 0.0)
        P.pool([], ratw, "memset", flat(RAT[32:33]), 1.0)
        P.pool([], ratw, "memset", flat(RAT[64:65]), 1.0)
        P.pool([], ["glS"], "memset", S, 0.0)
        P.pool([], [("glSb", 0), ("glSb", 1)], "memset", flat(Sba), 0.0)

        QGv = QG.rearrange("h p t -> p h t")
        KGv = KG.rearrange("h p t -> p h t")
        OBv = OB.rearrange("c p t -> p c t")
        GGv = GG.rearrange("c p t -> p c t")
        GAv = GA.rearrange("c p t -> p c t")
        h4 = lambda a: a.rearrange("p (h t) -> p h t", h=4)
        lastcol = 127 if dirn == 0 else 0

        groups = list(range(NTB)) if dirn == 0 else list(range(NTB - 1, -1, -1))
        seq = []
        for gi, g in enumerate(groups):
            chunks = list(range(4)) if dirn == 0 else list(range(3, -1, -1))
            for idx, c in enumerate(chunks):
                seq.append((gi, g, c, idx == 0, idx == 3))
        BX, BD, BC, BS, BO, BK = 0, 1, 2, 3, 4, 6

        def stage_a(n):
            gi, g, c, first, last = seq[n]
            k2 = gi % 2
            c2 = n % 2
            tsl = slice(g * TB, (g + 1) * TB)
            cs = slice(c * 128, (c + 1) * 128)
            if first:
                P.dma(RAT[0:32, k2, :], RA[:, tsl], r=["RA"], w=[("glRAT", k2)])
                P.dma(QT[:, k2], QGv[:, :, tsl], r=["QG"], w=[("glQT", k2)])
                P.dma(KT[:, k2], KGv[:, :, tsl], r=["KG"], w=[("glKT", k2)])
                P.dma(Kt[:, k2], KGt[g * TB:(g + 1) * TB, :].rearrange("(c p) n -> p c n", p=128), r=["KGt"], w=[("glKt", k2)])
                P.dma(Vt[:, k2], VGt[g * TB:(g + 1) * TB, :].rearrange("(c p) n -> p c n", p=128), r=["VGt"], w=[("glVt", k2)])
            SP, ED, EP, EM = SPa[:, c2], EDa[:, c2], EPa[:, c2], EMa[:, c2]
            kst, qd, ki = ksta[:, c2], qda[:, c2], kia[:, c2]
            P.pe([("glRAT", k2), "glWU"], [("ps", BX)], "matmul", ps[BX], lhsT=RAT[0:65, k2, cs], rhs=WU[0:65, :],
                 start=True, stop=True)
            P.act([("ps", BX)], [("glSP", c2)], "activation", out=SP, in_=ps[BX], func=AF.Exp, scale=-1.0)
            P.act([("glSP", c2)], [("glSP", c2)], "activation", out=SP, in_=SP, func=AF.Ln, bias=1.0, scale=1.0)
            P.pe(["CT", ("glSP", c2)], [("ps", BD)], "matmul", ps[BD], lhsT=TS, rhs=SP, start=True, stop=True)
            for h in range(4):
                hs = slice(h * 128, (h + 1) * 128)
                P.pe(["CT", ("glSP", c2)], [("ps", BC)], "matmul", ps[BC][:, hs], lhsT=SP[:, hs], rhs=TI,
                     start=True, stop=True)
            P.act([("ps", BD)], [("glED", c2)], "activation", out=ED, in_=ps[BD], func=AF.Exp)
            P.act([("ps", BC)], [("glEP", c2)], "activation", out=EP, in_=ps[BC], func=AF.Exp)
            P.act([("ps", BC)], [("glEM", c2)], "activation", out=EM, in_=ps[BC], func=AF.Exp, scale=-1.0)
            P.pool([("glKt", k2), ("glED", c2)], [("glkst", c2)], "tensor_tensor", out=kst, in0=Kt[:, k2, c, :], in1=ED,
                   op=ALU.mult)
            P.dve([("glQT", k2), ("glEP", c2)], [("glqd", c2)], "tensor_tensor", out=h4(qd), in0=QT[:, k2, :, cs],
                  in1=h4(EP), op=ALU.mult)
            P.dve([("glKT", k2), ("glEM", c2)], [("glki", c2)], "tensor_tensor", out=h4(ki), in0=KT[:, k2, :, cs],
                  in1=h4(EM), op=ALU.mult)

        def stage_b(n):
            gi, g, c, first, last = seq[n]
            k2 = gi % 2
            c2 = n % 2
            tsl = slice(g * TB, (g + 1) * TB)
            cs = slice(c * 128, (c + 1) * 128)
            EP = EPa[:, c2]
            kst, qd, ki, sc = ksta[:, c2], qda[:, c2], kia[:, c2], sca[:, c2]
            Sb_prev = Sba[:, n % 2]
            sbp = ("glSb", n % 2)
            Sb_new = Sba[:, (n + 1) % 2]
            sbn = ("glSb", (n + 1) % 2)
            OT = OTa[:, k2]
            otb = ("glOT", k2)
            for h in range(4):
                hs = slice(h * 128, (h + 1) * 128)
                P.pe([("glki", c2), ("glqd", c2)], [("ps", BS)], "matmul", ps[BS][:, hs], lhsT=ki[:, hs], rhs=qd[:, hs],
                     start=True, stop=True)
            P.dve([("ps", BS), "CT"], [("glsc", c2)], "tensor_tensor", out=h4(sc), in0=h4(ps[BS]),
                  in1=MK.unsqueeze(1).broadcast_to([128, 4, 128]), op=ALU.mult)
            for h in range(4):
                hs = slice(h * 128, (h + 1) * 128)
                for dvc in range(2):
                    blk = h * 2 + dvc
                    bk = BO + blk // 4
                    osl = slice((blk % 4) * 128, (blk % 4 + 1) * 128)
                    vs = slice(h * 256 + dvc * 128, h * 256 + dvc * 128 + 128)
                    P.pe([("glVt", k2), ("glsc", c2)], [("ps", bk)], "matmul", ps[bk][:, osl], lhsT=Vt[:, k2, c, vs],
                         rhs=sc[:, hs], start=True, stop=False)
                    P.pe([sbp, ("glqd", c2)], [("ps", bk)], "matmul", ps[bk][:, osl], lhsT=Sb_prev[:, vs],
                         rhs=qd[:, hs], start=False, stop=True)
            for h in range(4):
                bk = BK + h // 2
                ksl = slice((h % 2) * 256, (h % 2 + 1) * 256)
                P.pe([("glkst", c2), ("glVt", k2)], [("ps", bk)], "matmul", ps[bk][:, ksl],
                     lhsT=kst[:, h * 128:(h + 1) * 128], rhs=Vt[:, k2, c, h * 256:(h + 1) * 256], start=True, stop=True)
            for h in range(4):
                bk = BK + h // 2
                ksl = slice((h % 2) * 256, (h % 2 + 1) * 256)
                ssl = slice(h * 256, (h + 1) * 256)
                P.dve(["glS", ("glEP", c2), ("ps", bk)], ["glS"], "scalar_tensor_tensor", out=S[:, ssl], in0=S[:, ssl],
                      scalar=EP[:, h * 128 + lastcol:h * 128 + lastcol + 1], in1=ps[bk][:, ksl],
                      op0=ALU.mult, op1=ALU.add)
            P.act(["glS"], [sbn], "copy", out=Sb_new, in_=S)
            for half in range(2):
                P.act([("ps", BO + half)], [otb], "copy", out=OT[:, half * 4:(half + 1) * 4, cs],
                      in_=ps[BO + half].rearrange("p (b t) -> p b t", b=4))
            if not last:
                return
            if dirn == 1:
                P.dma(OBv[:, :, tsl], OT, r=[otb], w=["OB"])
            else:
                P.dma(OBs, OBv[:, :, tsl], r=["OB"], w=["glOBs"])
                P.dma(SG, GGv[:, :, tsl], r=["GG"], w=["glSG"])
                P.pool([otb, "glOBs"], [otb], "tensor_tensor", out=flat(OT), in0=flat(OT), in1=flat(OBs), op=ALU.add)
                P.act([otb], ["glOBs"], "activation", out=flat(OBs), in_=flat(OT), func=AF.Square)
                for h in range(4):
                    bn = BX if h % 2 == 0 else BD
                    for dvc in range(2):
                        P.pe(["ONESF", "glOBs"], [("ps", bn)], "matmul", ps[bn], lhsT=ONESF, rhs=OBs[:, h * 2 + dvc, :],
                             start=(dvc == 0), stop=(dvc == 1))
                    rs = RSa[:, h]
                    rstd_from(bn, rs, ("glRS", h), 256)
                    for dvc in range(2):
                        blk = h * 2 + dvc
                        gc = GCOL["gla"] + dvc
                        P.dve([otb, ("glRS", h), "GV"], [otb], "scalar_tensor_tensor", out=OT[:, blk, :], in0=OT[:, blk, :],
                              scalar=GV[:, gc:gc + 1], in1=rs, op0=ALU.mult, op1=ALU.mult)
                P.pool([otb, "glSG"], ["glGO"], "tensor_tensor", out=flat(GO), in0=flat(OT), in1=flat(SG), op=ALU.mult)
                P.dma(GAv[:, :, tsl], GO, r=["glGO"], w=["GA"])

        stage_a(0)
        for n in range(len(seq)):
            if n + 1 < len(seq):
                stage_a(n + 1)
            stage_b(n)
        state["bank"] = 0

    def attention(l):
        cv = Carver(0, ARENA_KB)
        Ka = cv([2, T], BF16)
        Va = cv([32, 256], BF16)
        Qa = cv([2, TB], BF16)
        Pa = cv([4, TB], BF16)
        Ra = cv([2, TB], F32)
        Oa = cv([2, TB], BF16)
        ACa = cv([2, 2, TB], F32)
        P.dma(Ka, AK.rearrange("h p t -> p h t"), r=["AK"], w=["atK"])
        P.dma(Va, AVt.rearrange("(n p) c -> p n c", p=128), r=["AVt"], w=["atV"])
        PS_S = [0, 1, 2]
        PS_O = [3, 4]
        PS_D = [5, 6]
        u = 0
        scale = 128.0 ** -0.5
        NKT = T // 128
        for qb in range(NTB):
            tsl = slice(qb * TB, (qb + 1) * TB)
            for h in range(8):
                kvh = h // 4
                u2 = u % 2
                Q = Qa[:, u2]
                qbuf = ("atQ", u2)
                P.dma(Q, AQ[h, :, tsl], r=["AQ"], w=[qbuf])
                bo = PS_O[u2]
                bd = PS_D[u2]

                def qk(kt, Q=Q, qbuf=qbuf, kvh=kvh):
                    bs = PS_S[kt % 3]
                    P.pe(["atK", qbuf], [("ps", bs)], "matmul", ps[bs], lhsT=Ka[:, kvh, kt * 128:(kt + 1) * 128], rhs=Q,
                         start=True, stop=True)

                qk(0)
                qk(1)
                for kt in range(NKT):
                    bs = PS_S[kt % 3]
                    pt = Pa[:, kt % 4]
                    ptb = ("atP", kt % 4)
                    P.act([("ps", bs)], [ptb], "activation", out=pt, in_=ps[bs], func=AF.Exp, scale=scale)
                    if kt + 2 < NKT:
                        qk(kt + 2)
                    P.pe(["atV", ptb], [("ps", bo)], "matmul", ps[bo], lhsT=Va[:, kt, kvh * 128:(kvh + 1) * 128], rhs=pt,
                         start=(kt == 0), stop=(kt == NKT - 1))
                    eng = kt % 2
                    acc = ACa[:, u2, eng]
                    accb = ("atAC", u2, eng)
                    addop = P.dve if eng == 0 else P.pool
                    if kt < 2:
                        addop([ptb], [accb], "tensor_copy", out=acc, in_=pt)
                    else:
                        addop([ptb, accb], [accb], "tensor_tensor", out=acc, in0=acc, in1=pt, op=ALU.add)
                for eng in range(2):
                    P.pe(["ONESF", ("atAC", u2, eng)], [("ps", bd)], "matmul", ps[bd], lhsT=ONESF, rhs=ACa[:, u2, eng],
                         start=(eng == 0), stop=(eng == 1))
                rr = Ra[:, u2]
                oo = Oa[:, u2]
                P.dve([("ps", bd)], [("atR", u2)], "reciprocal", out=rr, in_=ps[bd])
                P.dve([("ps", bo), ("atR", u2)], [("atO", u2)], "tensor_tensor", out=oo, in0=ps[bo], in1=rr, op=ALU.mult)
                P.dma(AT[h, :, tsl], oo, r=[("atO", u2)], w=["AT"])
                u += 1
        state["bank"] = 0

    def load_X(src, KC, t0=0, ntok=T):
        Xv = XR[:, 0:KC * ntok].rearrange("p (c t) -> p c t", c=KC)
        for kc in range(KC):
            for tb in range(ntok // TB):
                P.dma(Xv[:, kc, tb * TB:(tb + 1) * TB], src[kc, :, t0 + tb * TB:t0 + (tb + 1) * TB],
                      r=[src.name], w=[("X", kc, tb)])
        return Xv

    def gemm_simple(Wd, KC, ncols, Xv, ntb, epi, slabcols=512):
        Wv = wview(Wd)

        def xfn(kc, tb):
            return Xv[:, kc, tb * TB:(tb + 1) * TB], ("X", kc, tb)
        specs = []
        ti0 = 0
        for s0 in range(0, ncols, slabcols):
            nsl = min(slabcols, ncols - s0)

            def body(wb, wbuf, nsl=nsl, ti0=ti0):
                ti = ti0
                for c0 in range(0, nsl, 128):
                    for tb in range(ntb):
                        bank = nbank()
                        mm_group(bank, 128, wb, wbuf, c0, KC, xfn, tb)
                        epi(ti, tb, bank)
                    ti += 1
            specs.append(([Wv[:, :, s0:s0 + nsl]], KC, nsl, body))
            ti0 += nsl // 128
        run_slabs(specs)

    def merge_phase(l):
        cv = Carver(0, LOW_KB)
        Gr = Ring([cv([TB], BF16) for _ in range(3)], "mgG")
        Mr = Ring([cv([TB], F32) for _ in range(3)], "mgM")
        Mbr = Ring([cv([TB], BF16) for _ in range(3)], "mgMb")
        Xv = load_X(GA, 8)

        def epi_a(ti, tb, bank):
            tsl = slice(tb * TB, (tb + 1) * TB)
            g_, gb_ = Gr.next()
            m_, mb_ = Mr.next()
            P.dma(g_, SGA[ti, :, tsl], r=["SGA"], w=[gb_])
            P.dve([("ps", bank), gb_], [mb_], "tensor_tensor", out=m_, in0=ps[bank], in1=g_, op=ALU.mult)
            P.dma(MXT[ti, :, tsl], m_, r=[mb_], w=["MXT"])
        gemm_simple(w_o_gla[l], 8, D, Xv, NTB, epi_a)
        Xv = load_X(AT, 8)

        def epi_b(ti, tb, bank):
            tsl = slice(tb * TB, (tb + 1) * TB)
            g_, gb_ = Gr.next()
            m_, mb_ = Mr.next()
            o_, ob_ = Mbr.next()
            P.dma(g_, SGB[ti, :, tsl], r=["SGB"], w=[gb_])
            P.dma(m_, MXT[ti, :, tsl], r=["MXT"], w=[mb_])
            P.dve([("ps", bank), gb_], [ob_], "tensor_tensor", out=o_, in0=ps[bank], in1=g_, op=ALU.mult)
            P.pool([ob_, mb_], [ob_], "tensor_tensor", out=o_, in0=o_, in1=m_, op=ALU.add)
            P.dma(MX[ti, :, tsl], o_, r=[ob_], w=["MX"])
        gemm_simple(w_o_attn[l], 8, D, Xv, NTB, epi_b)
        Xv = load_X(MX, 8)

        def epi_o(ti, tb, bank):
            tsl = slice(tb * TB, (tb + 1) * TB)
            m_, mb_ = Mr.next()
            P.act([("ps", bank)], [mb_], "copy", out=m_, in_=ps[bank])
            P.dma(Yd[ti, :, tsl], m_, r=[mb_], w=["Yd"])
        gemm_simple(w_out[l], 8, D, Xv, NTB, epi_o)

    def ffn_phase(l):
        cv = Carver(0, LOW_KB)
        Ar = Ring([cv([TB], F32) for _ in range(3)], "ffA")
        Hr = Ring([cv([TB], BF16) for _ in range(3)], "ffH")
        WD = cv([22, D], BF16)
        Wv = wview(w_ffn_in[l])
        Wdv = wview(w_ffn_out[l])
        specs = []
        dn = []
        for k0 in range(0, 22, 4):
            nk = min(4, 22 - k0)
            dn.append((k0, nk))
        for j0 in range(0, 22, 2):
            def body(wb, wbuf, j0=j0):
                for jj in range(2):
                    j = j0 + jj
                    for tb in range(NTB):
                        tsl = slice(tb * TB, (tb + 1) * TB)
                        b1 = nbank()
                        mm_group(b1, 128, wb, wbuf, jj * 128, 8, x8fn, tb)
                        b2 = nbank()
                        mm_group(b2, 128, wb, wbuf, 256 + jj * 128, 8, x8fn, tb)
                        a_, ab_ = Ar.next()
                        h_, hb_ = Hr.next()
                        P.act([("ps", b1)], [ab_], "activation", out=a_, in_=ps[b1], func=AF.Silu)
                        P.dve([("ps", b2), ab_], [hb_], "tensor_tensor", out=h_, in0=ps[b2], in1=a_, op=ALU.mult)
                        P.dma(HID[j, :, tsl], h_, r=[hb_], w=["HID"])
            specs.append(([Wv[:, :, j0 * 128:(j0 + 2) * 128], Wv[:, :, FH + j0 * 128:FH + (j0 + 2) * 128]], 8, 512, body))
        nxt = load_slab(*specs[0][:3])
        for i, sp in enumerate(specs):
            cur = nxt
            if i < len(dn):
                k0, nk = dn[i]
                load_slab([Wdv[:, k0:k0 + nk, :]], nk, D, dst=WD[:, k0:k0 + nk, :], dstbuf=("ffWD", k0))
            if i + 1 < len(specs):
                nxt = load_slab(*specs[i + 1][:3])
            sp[3](*cur)
        XH = XR[:, 0:2 * 22 * TB].rearrange("p (a c t) -> p a c t", a=2, c=22)
        HIDv = HID.rearrange("c p t -> p c t")
        xr_all = [("X", kc, tb) for kc in range(8) for tb in range(NTB)]
        for tb in range(NTB):
            k2 = tb % 2
            tsl = slice(tb * TB, (tb + 1) * TB)
            P.dma(XH[:, k2], HIDv[:, :, tsl], r=["HID"], w=[("ffXH", k2)] + (xr_all if tb < 2 else []))
            for ct in range(8):
                bank = nbank()
                for kc in range(22):
                    P.pe([("ffWD", (kc // 4) * 4), ("ffXH", k2)], [("ps", bank)], "matmul", ps[bank],
                         lhsT=WD[:, kc, ct * 128:(ct + 1) * 128], rhs=XH[:, k2, kc, :], start=(kc == 0), stop=(kc == 21))
                m_, mb_ = Ar.next()
                P.act([("ps", bank)], [mb_], "copy", out=m_, in_=ps[bank])
                P.dma(Yd[ct, :, tsl], m_, r=[mb_], w=["Yd"])

    def ple_phase(l):
        cv = Carver(0, LOW_KB)
        PT = cv([2, T], BF16)
        pin = cv([2, PLE], F32)
        WP = cv([2, D], BF16)
        Sr = Ring([cv([TB], F32) for _ in range(3)], "plS")
        for tb in range(NTB):
            bks = [nbank(), nbank()]
            for tt in range(4):
                r0 = tb * TB + tt * 128
                k2 = tt % 2
                P.dma(pin[:, k2], p_in[l, r0:r0 + 128, :], w=[("plpin", k2)])
                for c in range(2):
                    P.pe([("plpin", k2), "CT"], [("ps", bks[c])], "transpose",
                         ps[bks[c]][:, tt * 128:(tt + 1) * 128], pin[:, k2, c * 128:(c + 1) * 128], IDN)
            for c in range(2):
                P.act([("ps", bks[c])], [("plPT", c, tb)], "copy", out=PT[:, c, tb * TB:(tb + 1) * TB], in_=ps[bks[c]])
        for s0 in range(0, D, 512):
            load_slab([wview(w_ple_proj[l])[:, :, s0:s0 + 512]], 2, 512, dst=WP[:, :, s0:s0 + 512], dstbuf="plWP")

        def ptfn(kc, tb):
            return PT[:, kc, tb * TB:(tb + 1) * TB], ("plPT", kc, tb)

        def epi(ti, tb, bank):
            tsl = slice(tb * TB, (tb + 1) * TB)
            b2 = nbank()
            mm_group(b2, 128, WP, "plWP", ti * 128, 2, ptfn, tb)
            s_, sb_ = Sr.next()
            P.act([("ps", bank)], [sb_], "activation", out=s_, in_=ps[bank], func=AF.Sigmoid)
            P.dve([("ps", b2), sb_], [sb_], "tensor_tensor", out=s_, in0=ps[b2], in1=s_, op=ALU.mult)
            P.dma(Yd[ti, :, tsl], s_, r=[sb_], w=["Yd"])
        gemm_simple(w_ple_gate[l], 8, D, X8, NTB, epi)

    stop_after = [d for d in debug if d.startswith("stop:")]
    stop_after = stop_after[0][5:] if stop_after else None

    def done(tag):
        P.barrier()
        return stop_after == tag

    def forward():
        for l in range(n_layers):
            load_gains(l)
            P.barrier()
            if l == 0:
                norm_pass(l, None, GCOL["mix_pre"], first=True)
                if done("norm0"):
                    return
            inproj(l)
            if done("inproj"):
                return
            gla_sweep(l, 1)
            P.barrier()
            gla_sweep(l, 0)
            if done("gla"):
                return
            attention(l)
            if done("attn"):
                return
            merge_phase(l)
            P.barrier()
            norm_pass(l, GCOL["mix_post"], GCOL["ffn_pre"])
            if done("mix"):
                return
            ffn_phase(l)
            P.barrier()
            norm_pass(l, GCOL["ffn_post"], None, raw_next=True)
            if done("ffn"):
                return
            ple_phase(l)
            P.barrier()
            lastl = (l == n_layers - 1)
            norm_pass(l, GCOL["ple_post"], GCOL["mix_pre_next"], last=lastl)
            P.barrier()

    forward()
    P.emit()
    return nc, P


_CACHE = {}


def kernel(**inputs):
    consts = make_consts()
    if "nc" not in _CACHE:
        _CACHE["nc"] = build()[0]
    nc = _CACHE["nc"]
    f32 = lambda a: np.ascontiguousarray(np.asarray(a, dtype=np.float32))
    shared = {k: f32(v) for k, v in inputs.items() if k not in ("x", "p")}
    shared.update(consts)
    x = f32(inputs["x"])
    p = f32(inputs["p"])
    in_maps = []
    for b in range(8):
        m = dict(shared)
        m["x"] = np.ascontiguousarray(x[b])
        m["p"] = np.ascontiguousarray(p[:, b])
        in_maps.append(m)
    res = run_bass_kernel_spmd(nc, in_maps, core_ids=list(range(8)))
    return np.stack([np.asarray(r["out"], dtype=np.float32) for r in res.results], axis=0)
```
